# Optimizing a Trainium2 kernel written in Bass

```python
import jax, jax.numpy as jnp
from jax import lax
import numpy as np

D_MODEL = 1024
BATCH = 2
SEQ = 8192
DEPTH = 2

N_A_LAYERS = DEPTH // 2
N_B_LAYERS = DEPTH - N_A_LAYERS
FOX_HEADS = 16
FOX_HEAD_DIM = D_MODEL // FOX_HEADS
FOX_WIDTH = FOX_HEADS * FOX_HEAD_DIM
FOX_IN_COLS = 3 * FOX_WIDTH + FOX_HEADS
MLA_HEADS = 8
QK_NOPE_DIM = 128
QK_ROPE_DIM = 64
V_HEAD_DIM = 128
Q_LORA_RANK = 384
KV_LORA_RANK = 256
ROPE_BASE = 10000.0
D_FF = 4 * D_MODEL
Q_BLOCK = 128
EPS = 1e-6

kernel_name = "yoco_fox_mla_hybrid"


def rms_norm(x, g):
    xf = x.astype(jnp.float32)
    y = xf * lax.rsqrt(jnp.mean(xf * xf, axis=-1, keepdims=True) + EPS)
    return (y * g.astype(jnp.float32)).astype(x.dtype)


def sq_relu_mlp(h, w_up, w_down):
    return jnp.square(jax.nn.relu(h @ w_up)) @ w_down


def rope_tables(seq_len, dim):
    inv = 1.0 / (ROPE_BASE ** (jnp.arange(0, dim, 2, dtype=jnp.float32) / dim))
    ang = jnp.arange(seq_len, dtype=jnp.float32)[:, None] * inv[None, :]
    return jnp.cos(ang), jnp.sin(ang)


def apply_rope(t, cos, sin):
    cos = cos.astype(t.dtype)
    sin = sin.astype(t.dtype)
    half = t.shape[-1] // 2
    t1, t2 = t[..., :half], t[..., half:]
    return jnp.concatenate([t1 * cos - t2 * sin, t2 * cos + t1 * sin], axis=-1)


def causal_block_attention(logits_fn, q_parts, v):
    B, S = v.shape[0], v.shape[1]
    nb = S // Q_BLOCK
    blocks = tuple(jnp.moveaxis(t.reshape((B, nb, Q_BLOCK) + t.shape[2:]), 1, 0) for t in q_parts)
    kpos = jnp.arange(S)

    def one_block(args):
        i, qb = args
        logits = logits_fn(qb)
        qpos = i * Q_BLOCK + jnp.arange(Q_BLOCK)
        logits = jnp.where(kpos[None, :] <= qpos[:, None], logits, -jnp.inf)
        p = jax.nn.softmax(logits, axis=-1).astype(v.dtype)
        return jnp.einsum('bhqk,bkhd->bqhd', p, v)

    out = lax.map(one_block, (jnp.arange(nb), blocks))
    return jnp.moveaxis(out, 0, 1).reshape((B, S) + out.shape[3:])


def fox_mixer(h, w_in, b_f, w_out):
    B, S, _ = h.shape
    proj = h @ w_in
    q = proj[..., :FOX_WIDTH].reshape(B, S, FOX_HEADS, FOX_HEAD_DIM)
    k = proj[..., FOX_WIDTH:2 * FOX_WIDTH].reshape(B, S, FOX_HEADS, FOX_HEAD_DIM)
    v = proj[..., 2 * FOX_WIDTH:3 * FOX_WIDTH].reshape(B, S, FOX_HEADS, FOX_HEAD_DIM)
    f_logit = proj[..., 3 * FOX_WIDTH:].astype(jnp.float32) + b_f.astype(jnp.float32)
    cum = jnp.cumsum(jax.nn.log_sigmoid(f_logit), axis=1)
    c_keys = jnp.transpose(cum, (0, 2, 1))
    scale = FOX_HEAD_DIM ** -0.5

    def logits_fn(qb):
        q_blk, c_blk = qb
        s = jnp.einsum('bqhd,bkhd->bhqk', q_blk, k, preferred_element_type=jnp.float32) * scale
        return s + jnp.transpose(c_blk, (0, 2, 1))[..., None] - c_keys[:, :, None, :]

    ctx = causal_block_attention(logits_fn, (q, cum), v)
    return ctx.reshape(B, S, FOX_WIDTH) @ w_out


def mla_shared_kv(stream, kv_norm_g, w_kv_a, kv_a_norm_g, w_kv_b, cos, sin):
    B, S, _ = stream.shape
    src = rms_norm(stream, kv_norm_g)
    kv_a = src @ w_kv_a
    c_kv = rms_norm(kv_a[..., :KV_LORA_RANK], kv_a_norm_g)
    k_rope = apply_rope(kv_a[..., KV_LORA_RANK:], cos, sin)
    kv_b = (c_kv @ w_kv_b).reshape(B, S, MLA_HEADS, QK_NOPE_DIM + V_HEAD_DIM)
    k_nope = kv_b[..., :QK_NOPE_DIM]
    v = kv_b[..., QK_NOPE_DIM:]
    return k_nope, k_rope, v


def mla_mixer(h, w_q_a, q_a_norm_g, w_q_b, w_out, k_nope, k_rope, v, cos, sin):
    B, S, _ = h.shape
    c_q = rms_norm(h @ w_q_a, q_a_norm_g)
    q = (c_q @ w_q_b).reshape(B, S, MLA_HEADS, QK_NOPE_DIM + QK_ROPE_DIM)
    q_nope = q[..., :QK_NOPE_DIM]
    q_rope = apply_rope(q[..., QK_NOPE_DIM:], cos[:, None, :], sin[:, None, :])
    scale = (QK_NOPE_DIM + QK_ROPE_DIM) ** -0.5

    def logits_fn(qb):
        qn, qr = qb
        s = jnp.einsum('bqhd,bkhd->bhqk', qn, k_nope, preferred_element_type=jnp.float32)
        s = s + jnp.einsum('bqhr,bkr->bhqk', qr, k_rope, preferred_element_type=jnp.float32)
        return s * scale

    ctx = causal_block_attention(logits_fn, (q_nope, q_rope), v)
    return ctx.reshape(B, S, MLA_HEADS * V_HEAD_DIM) @ w_out


def setup_inputs(seed: int = 0) -> dict:
    key = jax.random.key(seed)
    ks = jax.random.split(key, 24)

    def w(k, shape, fan_in):
        return jax.random.normal(k, shape, jnp.float32) * (fan_in ** -0.5)

    def gain(k, shape):
        return 1.0 + 0.02 * jax.random.normal(k, shape, jnp.float32)

    return {
        "x": jax.random.normal(ks[0], (BATCH, SEQ, D_MODEL), jnp.float32),
        "norm_mix_g": gain(ks[1], (DEPTH, D_MODEL)),
        "norm_ffn_g": gain(ks[2], (DEPTH, D_MODEL)),
        "fox_w_in": w(ks[3], (N_A_LAYERS, D_MODEL, FOX_IN_COLS), D_MODEL),
        "fox_b_f": 1.0 + 0.1 * jax.random.normal(ks[4], (N_A_LAYERS, FOX_HEADS), jnp.float32),
        "fox_w_out": w(ks[5], (N_A_LAYERS, FOX_WIDTH, D_MODEL), FOX_WIDTH),
        "kv_norm_g": gain(ks[6], (D_MODEL,)),
        "mla_w_kv_a": w(ks[7], (D_MODEL, KV_LORA_RANK + QK_ROPE_DIM), D_MODEL),
        "mla_kv_a_norm_g": gain(ks[8], (KV_LORA_RANK,)),
        "mla_w_kv_b": w(ks[9], (KV_LORA_RANK, MLA_HEADS * (QK_NOPE_DIM + V_HEAD_DIM)), KV_LORA_RANK),
        "mla_w_q_a": w(ks[10], (N_B_LAYERS, D_MODEL, Q_LORA_RANK), D_MODEL),
        "mla_q_a_norm_g": gain(ks[11], (N_B_LAYERS, Q_LORA_RANK)),
        "mla_w_q_b": w(ks[12], (N_B_LAYERS, Q_LORA_RANK, MLA_HEADS * (QK_NOPE_DIM + QK_ROPE_DIM)), Q_LORA_RANK),
        "mla_w_out": w(ks[13], (N_B_LAYERS, MLA_HEADS * V_HEAD_DIM, D_MODEL), MLA_HEADS * V_HEAD_DIM),
        "ffn_w_up": w(ks[14], (DEPTH, D_MODEL, D_FF), D_MODEL),
        "ffn_w_down": w(ks[15], (DEPTH, D_FF, D_MODEL), D_FF),
        "final_norm_g": gain(ks[16], (D_MODEL,)),
    }


def reference(x, norm_mix_g, norm_ffn_g, fox_w_in, fox_b_f, fox_w_out, kv_norm_g,
              mla_w_kv_a, mla_kv_a_norm_g, mla_w_kv_b, mla_w_q_a, mla_q_a_norm_g,
              mla_w_q_b, mla_w_out, ffn_w_up, ffn_w_down, final_norm_g):
    S = x.shape[1]
    cos, sin = rope_tables(S, QK_ROPE_DIM)
    k_nope = k_rope = v_shared = None
    for layer in range(DEPTH):
        h = rms_norm(x, norm_mix_g[layer])
        if layer < N_A_LAYERS:
            x = x + fox_mixer(h, fox_w_in[layer], fox_b_f[layer], fox_w_out[layer])
        else:
            b = layer - N_A_LAYERS
            x = x + mla_mixer(h, mla_w_q_a[b], mla_q_a_norm_g[b], mla_w_q_b[b], mla_w_out[b],
                              k_nope, k_rope, v_shared, cos, sin)
        x = x + sq_relu_mlp(rms_norm(x, norm_ffn_g[layer]), ffn_w_up[layer], ffn_w_down[layer])
        if layer == N_A_LAYERS - 1:
            k_nope, k_rope, v_shared = mla_shared_kv(x, kv_norm_g, mla_w_kv_a, mla_kv_a_norm_g,
                                                     mla_w_kv_b, cos, sin)
    return rms_norm(x, final_norm_g)
```

```python
from contextlib import ExitStack
import numpy as np
import ml_dtypes
import concourse.bass as bass
import concourse.mybir as mybir
from concourse.bass_utils import run_bass_kernel_spmd

F32 = mybir.dt.float32
BF16 = mybir.dt.bfloat16
AF = mybir.ActivationFunctionType
ALU = mybir.AluOpType

D = 1024
NCH = 8
H1, HD1 = 16, 64
H2 = 8
DFF = 4096
EPS = 1e-6
NEG = -30000.0
R = 4


class Cfg:
    def __init__(self, S):
        self.S = S
        self.T = S // R
        self.NL = self.T // 128
        self.NQ = self.NL // 4
        self.NTT = self.T // 512


def gblock(j, l):
    return 8 * (l // 2) + (j if l % 2 == 0 else 7 - j)


class Buf:
    __slots__ = ("name", "writers", "readers", "old_readers", "fullw", "dsem")

    def __init__(self, name):
        self.name = name
        self.writers = []
        self.readers = []
        self.old_readers = []
        self.fullw = []
        self.dsem = None


class _Op:
    __slots__ = ("eng", "fn", "deps", "signal", "count", "dma", "dcount")


ENGS = ("pe", "act", "dve", "pool", "sp")


class Fw:
    def __init__(self, nc, es):
        self.nc = nc
        self.es = es
        self.ops = {e: [] for e in ENGS}
        self.esem = {e: es.enter_context(nc.semaphore("s_" + e)) for e in ENGS}
        self.dsems = []

    def buf(self, name):
        return Buf(name)

    def bufs(self, name, n):
        return [Buf("%s%d" % (name, i)) for i in range(n)]

    def _dsem(self, b):
        if b.dsem is None:
            h = self.es.enter_context(self.nc.semaphore("d_" + b.name))
            b.dsem = [h, 0]
            self.dsems.append(b.dsem)
        return b.dsem

    def share_dsem(self, bufs):
        d = self._dsem(bufs[0])
        for b in bufs[1:]:
            b.dsem = d

    def op(self, eng, fn, reads=(), writes=(), pwrites=(), dma=None):
        o = _Op()
        o.eng, o.fn, o.signal, o.count, o.dma, o.dcount = eng, fn, False, 0, None, 0
        deps = []
        for b in reads:
            deps.extend(b.writers)
        for b in writes:
            deps.extend(b.writers)
            deps.extend(b.readers)
            deps.extend(b.old_readers)
        for b in pwrites:
            if b.readers:
                b.old_readers = b.readers
                b.readers = []
                b.writers = []
                b.fullw = []
            deps.extend(b.old_readers)
            deps.extend(b.fullw)
        o.deps = deps
        if dma is not None:
            d = self._dsem(dma)
            d[1] += 16
            o.dma, o.dcount = d, d[1]
            tok = ("d", d, d[1])
        else:
            tok = ("e", o)
        for b in reads:
            b.readers.append(tok)
        for b in writes:
            b.writers = [tok]
            b.fullw = [tok]
            b.readers = []
            b.old_readers = []
        for b in pwrites:
            b.writers.append(tok)
        self.ops[eng].append(o)
        return o

    def barrier(self):
        lasts = []
        for e in ("pe", "act", "dve", "pool"):
            real = [o for o in self.ops[e] if o.fn is not None and o.dma is None]
            if real:
                lasts.append(("e", real[-1]))
        dtoks = [("d", d, d[1]) for d in self.dsems if d[1] > 0]
        for e in ENGS:
            o = _Op()
            o.eng, o.fn, o.signal, o.count, o.dma, o.dcount = e, None, False, 0, None, 0
            o.deps = [t for t in lasts if t[1].eng != e] + dtoks
            self.ops[e].append(o)

    def emit(self):
        nc = self.nc
        for e in ENGS:
            for o in self.ops[e]:
                for t in o.deps:
                    if t[0] == "e":
                        p = t[1]
                        if p.eng == o.eng and p.eng in ("pe", "sp"):
                            continue
                        p.signal = True
        for e in ENGS:
            n = 0
            for o in self.ops[e]:
                if o.signal and o.fn is None:
                    raise RuntimeError("barrier op used as producer")
                if o.signal:
                    n += 1
                    o.count = n
        with nc.Block() as block:
            def run(e):
                def body(eng):
                    waited = {}
                    for o in self.ops[e]:
                        need = {}
                        for t in o.deps:
                            if t[0] == "e":
                                p = t[1]
                                if p.eng == o.eng and p.eng in ("pe", "sp"):
                                    continue
                                key, h, v = ("e", p.eng), self.esem[p.eng], p.count
                            else:
                                key, h, v = ("d", id(t[1])), t[1][0], t[2]
                            if v > need.get(key, (None, 0))[1]:
                                need[key] = (h, v)
                        for key, (h, v) in need.items():
                            if waited.get(key, 0) < v:
                                eng.wait_ge(h, v)
                                waited[key] = v
                        if o.fn is None:
                            continue
                        ins = o.fn(eng)
                        if o.dma is not None:
                            ins.then_inc(o.dma[0], 16)
                        elif o.signal:
                            ins.then_inc(self.esem[e], 1)
                    if e == "sp":
                        for d in self.dsems:
                            if d[1] > 0:
                                eng.wait_ge(d[0], d[1])
                return body
            block.tensor(run("pe"))
            block.scalar(run("act"))
            block.vector(run("dve"))
            block.gpsimd(run("pool"))
            block.sync(run("sp"))


class _Scope:
    def __init__(self, cx):
        self.cx = cx

    def __enter__(self):
        self.saved = self.cx.es
        self.cx.es = ExitStack()
        return self

    def __exit__(self, *a):
        self.cx.fw.barrier()
        self.cx.es.close()
        self.cx.es = self.saved
        return False


class Ctx:
    def __init__(self):
        self.nc = bass.Bass("TRN2", target_bir_lowering=False)
        self.es = ExitStack()
        self.fw = Fw(self.nc, self.es)

    def sb(self, name, shape, dt):
        return self.es.enter_context(self.nc.sbuf_tensor(name, list(shape), dt))

    def scope(self):
        return _Scope(self)

    def ps(self, name, shape=(128, 512), dt=F32):
        return self.es.enter_context(self.nc.psum_tensor(name, list(shape), dt))

    def din(self, name, shape, dt):
        return self.nc.dram_tensor(name, list(shape), dt, kind="ExternalInput").ap()

    def dout(self, name, shape, dt):
        return self.nc.dram_tensor(name, list(shape), dt, kind="ExternalOutput").ap()

    def dint(self, name, shape, dt):
        return self.nc.dram_tensor(name, list(shape), dt).ap()

    def finish(self):
        self.fw.emit()
        self.es.close()
        return self.nc


class Common:
    def __init__(self, cx, cfg):
        self.cx, self.cfg = cx, cfg
        fw = cx.fw
        self.pst = cx.ps("psum_all", [128, 4096])
        self.bank = [self.pst[:, i * 512:(i + 1) * 512] for i in range(8)]
        self.bbank = fw.bufs("bank", 8)
        self.ident_d = cx.din("ident", [128, 128], F32)
        self.ident = cx.sb("ident_sb", [128, 128], F32)
        self.b_ident = fw.buf("ident")
        self.ones32 = cx.sb("ones32", [128, 128], F32)
        self.b_ones32 = fw.buf("ones32")
        self.onesb = cx.sb("onesb", [128, 128], BF16)
        self.b_onesb = fw.buf("onesb")
        idt, idd = self.ident, self.ident_d
        fw.op("sp", lambda e: e.dma_start(out=idt[:], in_=idd[:, :]), writes=[self.b_ident], dma=self.b_ident)
        o32, ob = self.ones32, self.onesb
        fw.op("pool", lambda e: e.memset(o32[:], 1.0), writes=[self.b_ones32])
        fw.op("pool", lambda e: e.memset(ob[:], 1.0), writes=[self.b_onesb])
        self._rr = 0

    def evac_engine(self):
        self._rr += 1
        return "dve" if self._rr % 2 else "act"


def load_gcol(cx, name, src_ap):
    fw = cx.fw
    t = cx.sb(name, [128, src_ap.shape[1]], F32)
    b = fw.buf(name)
    fw.op("sp", lambda e: e.dma_start(out=t[:], in_=src_ap[:, :]), writes=[b], dma=b)
    return t, b


def load_xT_from_tokens(cx, cm, x_d, xT, bxT):
    cfg, fw = cm.cfg, cx.fw
    with cx.scope():
        _load_xT_body(cx, cm, x_d, xT, bxT)


def _load_xT_body(cx, cm, x_d, xT, bxT):
    cfg, fw = cm.cfg, cx.fw
    stg = [cx.sb("xstg%d" % i, [128, D], F32) for i in range(2)]
    bstg = fw.bufs("xstg", 2)
    for l in range(cfg.NL):
        s, bs = stg[l % 2], bstg[l % 2]
        fw.op("sp", lambda e, s=s, l=l: e.dma_start(out=s[:], in_=x_d[l * 128:(l + 1) * 128, :]),
              writes=[bs], dma=bs)
        for half in range(2):
            bk = (2 * l + half) % 2
            pb, bpb = cm.bank[bk], cm.bbank[bk]
            for q in range(4):
                c = half * 4 + q
                fw.op("pe", lambda e, pb=pb, s=s, c=c, q=q: e.transpose(
                    pb[:, q * 128:(q + 1) * 128], s[:, c * 128:(c + 1) * 128], cm.ident[:]),
                    reads=[bs, cm.b_ident], writes=[bpb] if q == 0 else [], pwrites=[] if q == 0 else [bpb])
            eng = cm.evac_engine()
            if eng == "dve":
                fw.op("dve", lambda e, pb=pb, half=half, l=l: e.tensor_copy(
                    xT[:, half * 4:half * 4 + 4, l * 128:(l + 1) * 128],
                    pb[:, :].rearrange("p (q t) -> p q t", q=4)),
                    reads=[bpb], pwrites=[bxT[l // 4]])
            else:
                fw.op("act", lambda e, pb=pb, half=half, l=l: e.copy(
                    xT[:, half * 4:half * 4 + 4, l * 128:(l + 1) * 128],
                    pb[:, :].rearrange("p (q t) -> p q t", q=4)),
                    reads=[bpb], pwrites=[bxT[l // 4]])


def rmsnorm_T(cx, cm, srcT, bsrc, nchunks, nfeat, gcol, bg, outT, bout, tag):
    cfg, fw = cm.cfg, cx.fw
    sq = [cx.sb("%s_sq%d" % (tag, i), [128, 512], F32) for i in range(2)]
    bsq = fw.bufs(tag + "_sq", 2)
    rstd = [cx.sb("%s_rstd%d" % (tag, i), [128, 512], F32) for i in range(2)]
    brstd = fw.bufs(tag + "_rstd", 2)
    k = 0
    for tt in range(cfg.NTT):
        ts = slice(tt * 512, (tt + 1) * 512)
        bk = 2 + tt % 2
        pb, bpb = cm.bank[bk], cm.bbank[bk]
        for c in range(nchunks):
            s, bs = sq[k % 2], bsq[k % 2]
            k += 1
            fw.op("act", lambda e, s=s, c=c, ts=ts: e.activation(out=s[:], in_=srcT[:, c, ts], func=AF.Square),
                  reads=[bsrc[tt]], writes=[bs])
            fw.op("pe", lambda e, pb=pb, s=s, c=c: e.matmul(pb[:, :], lhsT=cm.ones32[:], rhs=s[:],
                                                             start=(c == 0), stop=(c == nchunks - 1)),
                  reads=[bs, cm.b_ones32], writes=[bpb] if c == 0 else [], pwrites=[] if c == 0 else [bpb])
        r, br = rstd[tt % 2], brstd[tt % 2]
        fw.op("act", lambda e, r=r, pb=pb: e.activation(out=r[:], in_=pb[:, :], func=AF.Ln,
                                                        scale=1.0 / nfeat, bias=cm_eps(cx, cm)[:, 0:1]),
              reads=[bpb, cm.b_eps], writes=[br])
        fw.op("act", lambda e, r=r: e.activation(out=r[:], in_=r[:], func=AF.Exp, scale=-0.5),
              reads=[br], writes=[br])
        for c in range(nchunks):
            fw.op("dve", lambda e, c=c, ts=ts, r=r: e.scalar_tensor_tensor(
                out=outT[:, c, ts], in0=srcT[:, c, ts], scalar=gcol[:, c:c + 1], in1=r[:],
                op0=ALU.mult, op1=ALU.mult),
                reads=[bsrc[tt], bg, br], pwrites=[bout[tt]])


def cm_eps(cx, cm):
    if not hasattr(cm, "eps"):
        cm.eps = cx.sb("eps_t", [128, 1], F32)
        cm.b_eps = cx.fw.buf("eps")
        t = cm.eps
        cx.fw.op("pool", lambda e: e.memset(t[:], EPS), writes=[cm.b_eps])
    return cm.eps


def build_p1(cfg):
    cx = Ctx()
    fw = cx.fw
    cm = Common(cx, cfg)
    cm_eps(cx, cm)
    T, NL = cfg.T, cfg.NL
    x_d = cx.din("x", [T, D], F32)
    g_d = cx.din("g_mix", [128, NCH], F32)
    w_d = cx.din("w_in", [D, 3 * D + H1], F32)
    bf_d = cx.din("b_f", [H1, 1], F32)
    qT_d = cx.dout("qT", [D, T], BF16)
    kT_d = cx.dout("kT", [D, T], BF16)
    v_d = cx.dout("v", [H1, 128, NL, 65], BF16)
    lf_d = cx.dout("logf", [H1, T], F32)

    xT = cx.sb("xT_sb", [128, NCH, T], F32)
    bxT = fw.bufs("xT", cfg.NTT)
    load_xT_from_tokens(cx, cm, x_d, xT, bxT)
    gcol, bg = load_gcol(cx, "gmix", g_d)
    xn = cx.sb("xn", [128, NCH, T], BF16)
    bxn = fw.bufs("xn", cfg.NTT)
    rmsnorm_T(cx, cm, xT, bxT, NCH, D, gcol, bg, xn, bxn, "n1")
    fox_inproj(cx, cm, xn, bxn, w_d, bf_d, qT_d, kT_d, v_d, lf_d)
    return cx.finish()


def fox_inproj(cx, cm, xn, bxn, w_d, bf_d, qT_d, kT_d, v_d, lf_d):
    cfg, fw = cm.cfg, cx.fw
    T, NL = cfg.T, cfg.NL
    NW = 3 * D + H1
    w = cx.sb("w_in_sb", [128, NCH, NW], BF16)
    bw = fw.bufs("w_in", NCH)
    wv = w_d.rearrange("(c p) n -> p c n", p=128)
    for c in range(NCH):
        fw.op("pool", lambda e, c=c: e.dma_start(out=w[:, c, :], in_=wv[:, c, :]), writes=[bw[c]], dma=bw[c])
    nbf = cx.sb("nbf", [H1, 1], F32)
    bnbf = fw.buf("nbf")
    fw.op("sp", lambda e: e.dma_start(out=nbf[:], in_=bf_d[:, :]), writes=[bnbf], dma=bnbf)
    fw.op("dve", lambda e: e.tensor_scalar(out=nbf[:], in0=nbf[:], scalar1=-1.0, scalar2=None, op0=ALU.mult),
          reads=[bnbf], writes=[bnbf])
    ost = [cx.sb("qk_o%d" % i, [128, 512], BF16) for i in range(3)]
    bost = fw.bufs("qk_o", 3)
    k = 0
    for which, dst, scl in ((0, qT_d, 0.125), (1, kT_d, 1.0)):
        for m in range(8):
            col0 = which * D + m * 128
            for tt in range(cfg.NTT):
                ts = slice(tt * 512, (tt + 1) * 512)
                bk = 4 + k % 2
                pb, bpb = cm.bank[bk], cm.bbank[bk]
                for c in range(NCH):
                    fw.op("pe", lambda e, pb=pb, c=c, col0=col0, ts=ts: e.matmul(
                        pb[:, :], lhsT=w[:, c, col0:col0 + 128], rhs=xn[:, c, ts],
                        start=(c == 0), stop=(c == NCH - 1)),
                        reads=[bw[c], bxn[tt]], writes=[bpb] if c == 0 else [], pwrites=[] if c == 0 else [bpb])
                o, bo = ost[k % 3], bost[k % 3]
                if k % 2 == 0:
                    fw.op("dve", lambda e, o=o, pb=pb, scl=scl: e.tensor_scalar(
                        out=o[:], in0=pb[:, :], scalar1=scl, scalar2=None, op0=ALU.mult),
                        reads=[bpb], writes=[bo])
                else:
                    fw.op("act", lambda e, o=o, pb=pb, scl=scl: e.activation(
                        out=o[:], in_=pb[:, :], func=AF.Copy, scale=scl),
                        reads=[bpb], writes=[bo])
                fw.op("sp", lambda e, o=o, dst=dst, m=m, ts=ts: e.dma_start(
                    out=dst[m * 128:(m + 1) * 128, ts], in_=o[:]), reads=[bo], dma=bo)
                k += 1
    vst = [cx.sb("v_o%d" % i, [128, H1, 65], BF16) for i in range(2)]
    bvst = fw.bufs("v_o", 2)
    for i in range(2):
        fw.op("pool", lambda e, i=i: e.memset(vst[i][:, :, 64:65], 1.0), writes=[bvst[i]])
    for l in range(NL):
        o, bo = vst[l % 2], bvst[l % 2]
        for half in range(2):
            bk = 6 + half
            pb, bpb = cm.bank[bk], cm.bbank[bk]
            col0 = 2 * D + half * 512
            for c in range(NCH):
                fw.op("pe", lambda e, pb=pb, c=c, col0=col0, l=l: e.matmul(
                    pb[:, :], lhsT=xn[:, c, l * 128:(l + 1) * 128], rhs=w[:, c, col0:col0 + 512],
                    start=(c == 0), stop=(c == NCH - 1)),
                    reads=[bw[c], bxn[l // 4]], writes=[bpb] if c == 0 else [], pwrites=[] if c == 0 else [bpb])
            eng = "dve" if half == 0 else "act"
            if eng == "dve":
                fw.op("dve", lambda e, o=o, pb=pb, half=half: e.tensor_copy(
                    o[:, half * 8:half * 8 + 8, 0:64], pb[:, :].rearrange("p (h d) -> p h d", h=8)),
                    reads=[bpb], pwrites=[bo])
            else:
                fw.op("act", lambda e, o=o, pb=pb, half=half: e.copy(
                    o[:, half * 8:half * 8 + 8, 0:64], pb[:, :].rearrange("p (h d) -> p h d", h=8)),
                    reads=[bpb], pwrites=[bo])
        fw.op("sp", lambda e, o=o, l=l: e.dma_start(
            out=v_d[:, :, l, :].rearrange("h p d -> p h d"), in_=o[:]), reads=[bo], dma=bo)
    lf = cx.sb("lf_sb", [H1, T], F32)
    blf = fw.buf("lf")
    for tt in range(cfg.NTT):
        ts = slice(tt * 512, (tt + 1) * 512)
        bk = 4 + tt % 2
        pb, bpb = cm.bank[bk], cm.bbank[bk]
        for c in range(NCH):
            fw.op("pe", lambda e, pb=pb, c=c, ts=ts: e.matmul(
                pb[0:H1, :], lhsT=w[:, c, 3 * D:3 * D + H1], rhs=xn[:, c, ts],
                start=(c == 0), stop=(c == NCH - 1)),
                reads=[bw[c], bxn[tt]], writes=[bpb] if c == 0 else [], pwrites=[] if c == 0 else [bpb])
        fw.op("act", lambda e, pb=pb, ts=ts: e.activation(out=lf[:, ts], in_=pb[0:H1, :], func=AF.Exp,
                                                          scale=-1.0, bias=nbf[:, 0:1]),
              reads=[bpb, bnbf], pwrites=[blf])
        fw.op("act", lambda e, ts=ts: e.activation(out=lf[:, ts], in_=lf[:, ts], func=AF.Ln, bias=cm.ones32[0:H1, 0:1]),
              reads=[blf, cm.b_ones32], pwrites=[blf])
    fw.op("dve", lambda e: e.tensor_scalar(out=lf[:], in0=lf[:], scalar1=-1.0, scalar2=None, op0=ALU.mult),
          reads=[blf], writes=[blf])
    fw.op("sp", lambda e: e.dma_start(out=lf_d[:, :], in_=lf[:]), reads=[blf], dma=blf)


def core_tokens(cfg, j):
    idx = []
    for l in range(cfg.NL):
        g = gblock(j, l)
        idx.append(np.arange(g * 128, (g + 1) * 128))
    return np.concatenate(idx)


def gcols(g):
    n = g.shape[0] // 128
    return np.ascontiguousarray(g.reshape(n, 128).T.astype(np.float32))


_IDENT = np.eye(128, dtype=np.float32)


def p1_inputs(cfg, inp, c):
    b, j = c // R, c % R
    tok = core_tokens(cfg, j)
    return {
        "ident": _IDENT,
        "x": np.ascontiguousarray(inp["x"][b][tok]),
        "g_mix": gcols(inp["norm_mix_g"][0]),
        "w_in": np.ascontiguousarray(inp["fox_w_in"][0]),
        "b_f": np.ascontiguousarray(inp["fox_b_f"][0].reshape(H1, 1)),
    }


def mm(fw, out_ap, lhsT, rhs, first, last, reads, bout):
    fw.op("pe", lambda e: e.matmul(out_ap, lhsT=lhsT, rhs=rhs, start=first, stop=last),
          reads=reads, writes=[bout] if first else [], pwrites=[] if first else [bout])


def wload(cx, name, shape, src_view, npieces=1, eng="pool"):
    fw = cx.fw
    t = cx.sb(name, shape, BF16)
    b = fw.buf(name)
    n1 = shape[1]
    step = n1 // npieces
    for i in range(npieces):
        sl = slice(i * step, (i + 1) * step)
        if len(shape) == 2:
            fw.op(eng, lambda e, sl=sl: e.dma_start(out=t[:, sl], in_=src_view[:, sl]), pwrites=[b], dma=b)
        else:
            fw.op(eng, lambda e, sl=sl: e.dma_start(out=t[:, sl, :], in_=src_view[:, sl, :]), pwrites=[b], dma=b)
    return t, b


def attn_units(cfg, u):
    NL = cfg.NL
    units = []
    for l in range(4 * u):
        for r in range(R):
            units.append((r * NL + l, 0, None))
    for i in range(4):
        l = 4 * u + i
        for r in range(R):
            units.append((r * NL + l, 128 * i, (l % 2, r)))
    return units


def load_masks(cx, cm, masks_d):
    fw = cx.fw
    cm.masks = cx.sb("masks_sb", [128, 2, 4, 128], BF16)
    cm.b_masks = fw.buf("masks")
    fw.op("sp", lambda e: e.dma_start(out=cm.masks[:], in_=masks_d[:, :, :, :]), writes=[cm.b_masks], dma=cm.b_masks)
    cm.identb = cx.sb("identb", [128, 128], BF16)
    cm.b_identb = fw.buf("identb")
    fw.op("dve", lambda e: e.tensor_copy(cm.identb[:], cm.ident[:]), reads=[cm.b_ident], writes=[cm.b_identb])


def run_attention(cx, cm, spec):
    cfg, fw = cm.cfg, cx.fw
    sets = []
    for h in range(spec.nheads):
        for u in range(cfg.NQ):
            units = attn_units(cfg, u)
            grp = [units[i:i + 2] for i in range(0, len(units), 2)]
            for gi, g in enumerate(grp):
                sets.append((h, u, g, gi == 0, gi == len(grp) - 1))
    PT = [cx.sb("PT%d" % i, [128, 1024], BF16) for i in range(3)]
    bPT = fw.bufs("PT", 3)
    SS = [cm.pst[:, 0:1024], cm.pst[:, 1024:2048]]
    bSS = [(cm.bbank[0], cm.bbank[1]), (cm.bbank[2], cm.bbank[3])]
    pending = []

    def emit_qk(k):
        h, u, g, first, last = sets[k]
        for ui, unit in enumerate(g):
            spec.qk(h, u, unit, SS[k % 2][:, ui * 512:(ui + 1) * 512], bSS[k % 2][ui])

    def emit_rest(k):
        h, u, g, first, last = sets[k]
        pt, bpt = PT[k % 3], bPT[k % 3]
        ss, bss = SS[k % 2], bSS[k % 2]
        if len(g) == 2 and g[0][1] == 0 and g[1][1] == 0:
            spec.expf(pt[:, :], ss[:, :], [bss[0], bss[1]], bpt, True)
        else:
            for ui, unit in enumerate(g):
                c0 = ui * 512 + unit[1]
                c1 = (ui + 1) * 512
                spec.expf(pt[:, c0:c1], ss[:, c0:c1], [bss[ui]], bpt, ui == 0)
        for ui, unit in enumerate(g):
            c0 = ui * 512 + unit[1]
            c1 = (ui + 1) * 512
            spec.pv(h, u, unit, pt[:, c0:c1], bpt, first and ui == 0, last and ui == len(g) - 1)
        if last:
            spec.epi1(h, u)
            pending.append([k + 2, h, u])
        while pending and pending[0][0] <= k:
            _, ph, pu = pending.pop(0)
            spec.epi2(ph, pu)

    spec.load_head(0)
    if spec.nheads > 1:
        spec.load_head(1)
    emit_qk(0)
    for k in range(len(sets)):
        if k + 1 < len(sets):
            emit_qk(k + 1)
        emit_rest(k)
        h = sets[k][0]
        if (k + 1 == len(sets) or sets[k + 1][0] != h) and h + 2 < spec.nheads:
            spec.load_head(h + 2)
    while pending:
        _, ph, pu = pending.pop(0)
        spec.epi2(ph, pu)


def fox_cumgate(cx, cm, lf_all_d, tri_d, selneg_d, caug_d, cq_d):
    cfg, fw = cm.cfg, cx.fw
    NSEG = cfg.NL // 2
    P = 16 * NSEG
    with cx.scope():
        LG = cx.sb("LG", [P, 1024], F32)
        bLG = fw.buf("LG")
        ONES = cx.sb("cONES", [P, 1024], F32)
        bON = fw.buf("cONES")
        fw.op("pool", lambda e: e.memset(ONES[:], 1.0), writes=[bON])
        TRI = cx.sb("TRI", [P, P], F32)
        bTRI = fw.buf("TRI")
        fw.op("sp", lambda e: e.dma_start(out=TRI[:], in_=tri_d[:, :]), writes=[bTRI], dma=bTRI)
        SELN = cx.sb("SELN", [P, 4], F32)
        bSEL = fw.buf("SELN")
        fw.op("sp", lambda e: e.dma_start(out=SELN[:], in_=selneg_d[:, :]), writes=[bSEL], dma=bSEL)
        for r in range(R):
            for e_ in range(2):
                base = r if e_ == 0 else 7 - r
                fw.op("sp", lambda e, r=r, e_=e_, base=base: e.dma_start(
                    out=LG[:, base * 128:(base + 1) * 128], in_=lf_all_d[r, :, e_, :]), pwrites=[bLG], dma=bLG)
        C = cx.sb("Cc", [P, 1024], F32)
        bC = fw.buf("Cc")
        fw.op("dve", lambda e: e.tensor_tensor_scan(out=C[:], data0=ONES[:], data1=LG[:], initial=0.0,
                                                    op0=ALU.mult, op1=ALU.add),
              reads=[bON, bLG], writes=[bC])
        tot = cx.sb("ctot", [P, 2], F32)
        btot = fw.buf("ctot")
        fw.op("dve", lambda e: e.tensor_copy(tot[:, 0:1], C[:, 1023:1024]), reads=[bC], writes=[btot])
        fw.op("dve", lambda e: e.tensor_copy(tot[:, 1:2], C[:, 1023:1024]), reads=[bC], pwrites=[btot])
        pb, bpb = cm.bank[7], cm.bbank[7]
        mm(fw, pb[0:P, 0:2], TRI[:, :], tot[:, :], True, True, [bTRI, btot], bpb)
        off = cx.sb("coff", [P, 2], F32)
        boff = fw.buf("coff")
        fw.op("dve", lambda e: e.tensor_copy(off[:], pb[0:P, 0:2]), reads=[bpb], writes=[boff])
        NCv = cx.sb("cNC", [P, 1024], F32)
        bNC = fw.buf("cNC")
        fw.op("dve", lambda e: e.tensor_scalar(out=NCv[:], in0=C[:], scalar1=off[:, 0:1], scalar2=-1.0,
                                               op0=ALU.add, op1=ALU.mult),
              reads=[bC, boff], writes=[bNC])
        SP = [cx.sb("cSP%d" % i, [P, 1024], BF16) for i in range(3)]
        bSP = fw.bufs("cSP", 3)
        R1 = cx.sb("cR1", [P, 1024], F32)
        bR1 = fw.buf("cR1")
        fw.op("dve", lambda e: e.tensor_copy(SP[0][:], NCv[:]), reads=[bNC], writes=[bSP[0]])
        fw.op("dve", lambda e: e.tensor_tensor(out=R1[:], in0=NCv[:], in1=SP[0][:], op=ALU.subtract),
              reads=[bNC, bSP[0]], writes=[bR1])
        fw.op("dve", lambda e: e.tensor_copy(SP[1][:], R1[:]), reads=[bR1], writes=[bSP[1]])
        fw.op("dve", lambda e: e.tensor_tensor(out=NCv[:], in0=R1[:], in1=SP[1][:], op=ALU.subtract),
              reads=[bR1, bSP[1]], writes=[bNC])
        fw.op("dve", lambda e: e.tensor_copy(SP[2][:], NCv[:]), reads=[bNC], writes=[bSP[2]])
        bcaug = fw.buf("caug_d")
        bcq = fw.buf("cq_d")
        for i in range(3):
            for r in range(R):
                for e_ in range(2):
                    base = r if e_ == 0 else 7 - r
                    fw.op("sp", lambda e, i=i, r=r, e_=e_, base=base: e.dma_start(
                        out=caug_d[i, r, :, e_, :], in_=SP[i][:, base * 128:(base + 1) * 128]),
                        reads=[bSP[i]], pwrites=[bcaug], dma=bSP[i])
        OWN = [cx.sb("cOWN%d" % i, [P, 256], BF16) for i in range(3)]
        bOWN = fw.bufs("cOWN", 3)
        for i in range(3):
            for e_ in range(2):
                for r in range(R):
                    base = r if e_ == 0 else 7 - r
                    dst = OWN[i][:, e_ * 128:(e_ + 1) * 128]
                    src = SP[i][:, base * 128:(base + 1) * 128]
                    if r == 0:
                        fw.op("dve", lambda e, dst=dst, src=src, r=r: e.tensor_scalar(
                            out=dst, in0=src, scalar1=SELN[:, r:r + 1], scalar2=None, op0=ALU.mult),
                            reads=[bSP[i], bSEL], pwrites=[bOWN[i]])
                    else:
                        fw.op("dve", lambda e, dst=dst, src=src, r=r: e.scalar_tensor_tensor(
                            out=dst, in0=src, scalar=SELN[:, r:r + 1], in1=dst, op0=ALU.mult, op1=ALU.add),
                            reads=[bSP[i], bSEL, bOWN[i]], pwrites=[bOWN[i]])
            fw.op("sp", lambda e, i=i: e.dma_start(
                out=cq_d[i, :, :, :], in_=OWN[i][:, :].rearrange("p (e t) -> p e t", e=2)),
                reads=[bOWN[i]], pwrites=[bcq], dma=bOWN[i])
    return bcaug, bcq


class FoxSpec:
    nheads = H1

    def __init__(self, cx, cm, qT_d, kT_all_d, v_all_d, caug_d, cq_d, bcaug, bcq, CTX, bCTX):
        cfg, fw = cm.cfg, cx.fw
        self.cx, self.cm, self.cfg, self.fw = cx, cm, cfg, fw
        T, S, NL = cfg.T, cfg.S, cfg.NL
        self.qT_d, self.kT_all_d, self.v_all_d, self.caug_d, self.cq_d = qT_d, kT_all_d, v_all_d, caug_d, cq_d
        self.bcaug, self.bcq = bcaug, bcq
        self.CTX, self.bCTX = CTX, bCTX
        self.KA = [cx.sb("KA%d" % i, [128, S], BF16) for i in range(2)]
        self.QA = [cx.sb("QA%d" % i, [128, T], BF16) for i in range(2)]
        self.VA = [cx.sb("VA%d" % i, [128, R * NL, 65], BF16) for i in range(2)]
        self.bKA, self.bQA, self.bVA = fw.bufs("KA", 2), fw.bufs("QA", 2), fw.bufs("VA", 2)
        for i in range(2):
            KA, QA = self.KA[i], self.QA[i]
            fw.op("pool", lambda e, KA=KA: e.memset(KA[:], 0.0), writes=[self.bKA[i]])
            fw.op("pool", lambda e, KA=KA: e.memset(KA[64:67, :], 1.0), writes=[self.bKA[i]])
            fw.op("pool", lambda e, QA=QA: e.memset(QA[:], 0.0), writes=[self.bQA[i]])
            fw.op("pool", lambda e, QA=QA: e.memset(QA[64:70, :], 1.0), writes=[self.bQA[i]])
        self.OUT = [cm.bank[4], cm.bank[5]]
        self.bOUT = [cm.bbank[4], cm.bbank[5]]
        self.BC, self.bBC = cm.bank[6], cm.bbank[6]
        self.SEL64 = cx.sb("SEL64", [128, 64], F32)
        self.bSEL64 = fw.buf("SEL64")
        fw.op("pool", lambda e: e.memset(self.SEL64[:], 0.0), writes=[self.bSEL64])
        fw.op("pool", lambda e: e.memset(self.SEL64[64:65, :], 1.0), writes=[self.bSEL64])
        self.rden = [cx.sb("rden%d" % i, [128, 512], F32) for i in range(2)]
        self.brden = fw.bufs("rden", 2)
        self.bcs = [cx.sb("bcs%d" % i, [64, 512], F32) for i in range(2)]
        self.bbcs = fw.bufs("bcs", 2)
        for i in range(2):
            rd = self.rden[i]
            fw.op("pool", lambda e, rd=rd: e.memset(rd[:], 0.0), writes=[self.brden[i]])

    def load_head(self, h):
        fw, cfg = self.fw, self.cfg
        s = h % 2
        KA, QA, VA = self.KA[s], self.QA[s], self.VA[s]
        T = cfg.T
        fw.op("sp", lambda e: e.dma_start(
            out=KA[0:64, :].rearrange("p (r t) -> p r t", r=R),
            in_=self.kT_all_d[:, h * 64:(h + 1) * 64, :].rearrange("r p t -> p r t")),
            pwrites=[self.bKA[s]], dma=self.bKA[s])
        fw.op("sp", lambda e: e.dma_start(
            out=KA[67:70, :].rearrange("p (r t) -> p r t", r=R),
            in_=self.caug_d[:, :, h * (T // 256):(h + 1) * (T // 256), :, :].rearrange("i r m e p -> i r (m e p)")),
            reads=[self.bcaug], pwrites=[self.bKA[s]], dma=self.bKA[s])
        fw.op("sp", lambda e: e.dma_start(out=QA[0:64, :], in_=self.qT_d[h * 64:(h + 1) * 64, :]),
              pwrites=[self.bQA[s]], dma=self.bQA[s])
        fw.op("sp", lambda e: e.dma_start(
            out=QA[64:67, :],
            in_=self.cq_d[:, h * (T // 256):(h + 1) * (T // 256), :, :].rearrange("i m e p -> i (m e p)")),
            reads=[self.bcq], pwrites=[self.bQA[s]], dma=self.bQA[s])
        fw.op("sp", lambda e: e.dma_start(
            out=VA[:, :, :].rearrange("p (r l) d -> p r (l d)", r=R),
            in_=self.v_all_d[:, h, :, :, :].rearrange("r p l d -> p r (l d)")),
            pwrites=[self.bVA[s]], dma=self.bVA[s])

    def qk(self, h, u, unit, ss_ap, bss):
        fw, cm = self.fw, self.cm
        s = h % 2
        kb, c0, msk = unit
        KA, QA = self.KA[s], self.QA[s]
        q0 = u * 512 + c0
        q1 = (u + 1) * 512
        mm(fw, ss_ap[:, c0:512], KA[:, kb * 128:(kb + 1) * 128], QA[:, q0:q1], True, msk is None,
           [self.bKA[s], self.bQA[s]], bss)
        if msk is not None:
            mm(fw, ss_ap[:, c0:c0 + 128], cm.identb[:, :], cm.masks[:, msk[0], msk[1], :], False, True,
               [cm.b_identb, cm.b_masks], bss)

    def expf(self, out_ap, in_ap, bins, bpt, first):
        self.fw.op("act", lambda e: e.activation(out=out_ap, in_=in_ap, func=AF.Exp),
                   reads=bins, writes=[bpt] if first else [], pwrites=[] if first else [bpt])

    def pv(self, h, u, unit, pt_ap, bpt, first, last):
        s = h % 2
        ob = (h * self.cfg.NQ + u) % 2
        kb, c0, msk = unit
        mm(self.fw, self.OUT[ob][0:65, c0:512], self.VA[s][:, kb, 0:65], pt_ap, first, last,
           [self.bVA[s], bpt], self.bOUT[ob])

    def epi1(self, h, u):
        ob = (h * self.cfg.NQ + u) % 2
        rd, OUT = self.rden[ob], self.OUT[ob]
        self.fw.op("dve", lambda e: e.reciprocal(rd[64:65, :], OUT[64:65, :]),
                   reads=[self.bOUT[ob]], pwrites=[self.brden[ob]])

    def epi2(self, h, u):
        fw = self.fw
        ob = (h * self.cfg.NQ + u) % 2
        rd, OUT, bcs = self.rden[ob], self.OUT[ob], self.bcs[ob]
        mm(fw, self.BC[0:64, :], self.SEL64[64:96, :], rd[64:96, :], True, True,
           [self.bSEL64, self.brden[ob]], self.bBC)
        fw.op("dve", lambda e: e.tensor_copy(bcs[:, :], self.BC[0:64, :]), reads=[self.bBC], writes=[self.bbcs[ob]])
        CTX = self.CTX
        fw.op("dve", lambda e: e.tensor_tensor(out=CTX[0:64, h, u * 512:(u + 1) * 512], in0=OUT[0:64, :],
                                               in1=bcs[:, :], op=ALU.mult),
              reads=[self.bOUT[ob], self.bbcs[ob]], pwrites=[self.bCTX])


def out_proj(cx, cm, CTX, bCTX, kparts, npart, w_d, xT, bxT, tag):
    cfg, fw = cm.cfg, cx.fw
    with cx.scope():
        wo, bwo = wload(cx, tag + "_wo", [kparts, npart, D], w_d.rearrange("(k p) n -> p k n", p=kparts), npieces=4)
        k = 0
        for tt in range(cfg.NTT):
            ts = slice(tt * 512, (tt + 1) * 512)
            for dm in range(NCH):
                bk = k % 2
                k += 1
                pb, bpb = cm.bank[bk], cm.bbank[bk]
                for kk in range(npart):
                    mm(fw, pb[:, :], wo[:, kk, dm * 128:(dm + 1) * 128], CTX[0:kparts, kk, ts],
                       kk == 0, kk == npart - 1, [bwo, bCTX], bpb)
                fw.op("dve", lambda e, pb=pb, dm=dm, ts=ts: e.tensor_tensor(
                    out=xT[:, dm, ts], in0=pb[:, :], in1=xT[:, dm, ts], op=ALU.add),
                    reads=[bpb, bxT[tt]], pwrites=[bxT[tt]])


def ffn(cx, cm, xT, bxT, g_d, wup_d, wdn_d, tag):
    cfg, fw = cm.cfg, cx.fw
    T = cfg.T
    FB = 512
    NFB = DFF // FB
    with cx.scope():
        gcol, bg = load_gcol(cx, tag + "_g", g_d)
        xn = cx.sb(tag + "_xn", [128, NCH, T], BF16)
        bxn = fw.bufs(tag + "_xn", cfg.NTT)
        rmsnorm_T(cx, cm, xT, bxT, NCH, D, gcol, bg, xn, bxn, tag + "_n")
        wu = [cx.sb("%s_wu%d" % (tag, i), [128, NCH, FB], BF16) for i in range(2)]
        wd = [cx.sb("%s_wd%d" % (tag, i), [128, FB // 128, D], BF16) for i in range(2)]
        bwu, bwd = fw.bufs(tag + "_wu", 2), fw.bufs(tag + "_wd", 2)
        hT = [cx.sb("%s_hT%d" % (tag, i), [128, FB // 128, 512], BF16) for i in range(2)]
        bhT = fw.bufs(tag + "_hT", 2)
        rl = [cx.sb("%s_rl%d" % (tag, i), [128, 512], F32) for i in range(2)]
        brl = fw.bufs(tag + "_rl", 2)
        wuv = wup_d.rearrange("(c p) f -> p c f", p=128)
        wdv = wdn_d.rearrange("(c p) n -> p c n", p=128)

        def loadw(F):
            s = F % 2
            for half in range(2):
                hs = slice(half * 4, half * 4 + 4)
                fw.op("pool", lambda e, hs=hs: e.dma_start(out=wu[s][:, hs, :], in_=wuv[:, hs, F * FB:(F + 1) * FB]),
                      writes=[bwu[s]] if half == 0 else [], pwrites=[] if half == 0 else [bwu[s]], dma=bwu[s])
            fw.op("pool", lambda e: e.dma_start(out=wd[s][:, :, :], in_=wdv[:, F * 4:(F + 1) * 4, :]),
                  writes=[bwd[s]], dma=bwd[s])

        steps = [(F, tt) for F in range(NFB) for tt in range(cfg.NTT)]
        cnt = [0, 0]

        def up(i):
            F, tt = steps[i]
            s = F % 2
            ts = slice(tt * 512, (tt + 1) * 512)
            for fc in range(FB // 128):
                bk = cnt[0] % 2
                cnt[0] += 1
                pb, bpb = cm.bank[bk], cm.bbank[bk]
                for c in range(NCH):
                    mm(fw, pb[:, :], wu[s][:, c, fc * 128:(fc + 1) * 128], xn[:, c, ts], c == 0, c == NCH - 1,
                       [bwu[s], bxn[tt]], bpb)
                r, br = rl[bk], brl[bk]
                fw.op("act", lambda e, r=r, pb=pb: e.activation(out=r[:], in_=pb[:, :], func=AF.Relu),
                      reads=[bpb], writes=[br])
                h_ = hT[i % 2]
                fw.op("pool", lambda e, r=r, h_=h_, fc=fc: e.tensor_tensor(out=h_[:, fc, :], in0=r[:], in1=r[:],
                                                                           op=ALU.mult),
                      reads=[br], writes=[bhT[i % 2]] if fc == 0 else [], pwrites=[] if fc == 0 else [bhT[i % 2]])

        def down(i):
            F, tt = steps[i]
            s = F % 2
            ts = slice(tt * 512, (tt + 1) * 512)
            for dm in range(NCH):
                bk = 2 + cnt[1] % 2
                cnt[1] += 1
                pb, bpb = cm.bank[bk], cm.bbank[bk]
                for fc in range(FB // 128):
                    mm(fw, pb[:, :], wd[s][:, fc, dm * 128:(dm + 1) * 128], hT[i % 2][:, fc, :], fc == 0,
                       fc == FB // 128 - 1, [bwd[s], bhT[i % 2]], bpb)
                fw.op("dve", lambda e, pb=pb, dm=dm, ts=ts: e.tensor_tensor(
                    out=xT[:, dm, ts], in0=pb[:, :], in1=xT[:, dm, ts], op=ALU.add),
                    reads=[bpb, bxT[tt]], pwrites=[bxT[tt]])

        loadw(0)
        if NFB > 1:
            loadw(1)
        up(0)
        for i in range(len(steps)):
            if i + 1 < len(steps):
                up(i + 1)
            down(i)
            F, tt = steps[i]
            if tt == cfg.NTT - 1 and F + 2 < NFB:
                loadw(F + 2)


def mla_kv(cx, cm, xT, bxT, g_d, wkva_d, gkva_d, wkvb_d, cs1_d, cs2_d, knT_d, krT_d, v2_d):
    cfg, fw = cm.cfg, cx.fw
    T, NL = cfg.T, cfg.NL
    with cx.scope():
        gcol, bg = load_gcol(cx, "kv_g", g_d)
        sn = cx.sb("kv_sn", [128, NCH, T], BF16)
        bsn = fw.bufs("kv_sn", cfg.NTT)
        rmsnorm_T(cx, cm, xT, bxT, NCH, D, gcol, bg, sn, bsn, "kv_n")
        wv = wkva_d.rearrange("(c p) n -> p c n", p=128)
        wa, bwa = wload(cx, "kv_wa", [128, NCH, 320], wv)
        wsw = cx.sb("kv_wsw", [128, NCH, 64], BF16)
        bwsw = fw.buf("kv_wsw")
        fw.op("pool", lambda e: e.dma_start(out=wsw[:, :, 0:32], in_=wv[:, :, 288:320]), pwrites=[bwsw], dma=bwsw)
        fw.op("pool", lambda e: e.dma_start(out=wsw[:, :, 32:64], in_=wv[:, :, 256:288]), pwrites=[bwsw], dma=bwsw)
        cs1 = cx.sb("kv_cs1", [64, T], F32)
        cs2 = cx.sb("kv_cs2", [64, T], F32)
        bcs = fw.buf("kv_cs")
        fw.op("sp", lambda e: e.dma_start(out=cs1[:], in_=cs1_d[0:64, :]), pwrites=[bcs], dma=bcs)
        fw.op("sp", lambda e: e.dma_start(out=cs2[:], in_=cs2_d[0:64, :]), pwrites=[bcs], dma=bcs)
        ckv = cx.sb("kv_ckv", [128, 2, T], F32)
        bckv = fw.bufs("kv_ckv", cfg.NTT)
        kr = cx.sb("kv_kr", [64, T], BF16)
        bkr = fw.buf("kv_kr")
        t1 = cx.sb("kv_t1", [64, 512], F32)
        t2 = cx.sb("kv_t2", [64, 512], F32)
        bt1, bt2 = fw.buf("kv_t1"), fw.buf("kv_t2")
        k = 0
        for tt in range(cfg.NTT):
            ts = slice(tt * 512, (tt + 1) * 512)
            for m in range(2):
                pb, bpb = cm.bank[k % 2], cm.bbank[k % 2]
                k += 1
                for c in range(NCH):
                    mm(fw, pb[:, :], wa[:, c, m * 128:(m + 1) * 128], sn[:, c, ts], c == 0, c == NCH - 1,
                       [bwa, bsn[tt]], bpb)
                fw.op("act", lambda e, pb=pb, m=m, ts=ts: e.copy(ckv[:, m, ts], pb[:, :]),
                      reads=[bpb], pwrites=[bckv[tt]])
            pa, bpa = cm.bank[2], cm.bbank[2]
            pc, bpc = cm.bank[3], cm.bbank[3]
            for c in range(NCH):
                mm(fw, pa[0:64, :], wa[:, c, 256:320], sn[:, c, ts], c == 0, c == NCH - 1, [bwa, bsn[tt]], bpa)
            for c in range(NCH):
                mm(fw, pc[0:64, :], wsw[:, c, :], sn[:, c, ts], c == 0, c == NCH - 1, [bwsw, bsn[tt]], bpc)
            fw.op("dve", lambda e, ts=ts: e.tensor_tensor(out=t1[:], in0=pa[0:64, :], in1=cs1[:, ts], op=ALU.mult),
                  reads=[bpa, bcs], writes=[bt1])
            fw.op("dve", lambda e, ts=ts: e.tensor_tensor(out=t2[:], in0=pc[0:64, :], in1=cs2[:, ts], op=ALU.mult),
                  reads=[bpc, bcs], writes=[bt2])
            fw.op("dve", lambda e, ts=ts: e.tensor_tensor(out=kr[:, ts], in0=t1[:], in1=t2[:], op=ALU.add),
                  reads=[bt1, bt2], pwrites=[bkr])
        fw.op("sp", lambda e: e.dma_start(out=krT_d[:, :], in_=kr[:]), reads=[bkr], dma=bkr)
        gk, bgk = load_gcol(cx, "kva_g", gkva_d)
        ckvn = cx.sb("kv_ckvn", [128, 2, T], BF16)
        bckvn = fw.bufs("kv_ckvn", cfg.NTT)
        rmsnorm_T(cx, cm, ckv, bckv, 2, 256, gk, bgk, ckvn, bckvn, "kva_n")
        wb5 = wkvb_d.rearrange("(c p) (h t d) -> p c h t d", p=128, t=2, d=128)
        wkn = cx.sb("kv_wkn", [128, 2, H2, 128], BF16)
        wvv = cx.sb("kv_wv", [128, 2, H2, 128], BF16)
        bwkn, bwvv = fw.buf("kv_wkn"), fw.buf("kv_wv")
        for c in range(2):
            fw.op("pool", lambda e, c=c: e.dma_start(out=wkn[:, c, :, :], in_=wb5[:, c, :, 0, :]), pwrites=[bwkn], dma=bwkn)
            fw.op("pool", lambda e, c=c: e.dma_start(out=wvv[:, c, :, :], in_=wb5[:, c, :, 1, :]), pwrites=[bwvv], dma=bwvv)
        ost = [cx.sb("kv_o%d" % i, [128, 512], BF16) for i in range(2)]
        bost = fw.bufs("kv_o", 2)
        k = 0
        for h in range(H2):
            for tt in range(cfg.NTT):
                ts = slice(tt * 512, (tt + 1) * 512)
                pb, bpb = cm.bank[k % 2], cm.bbank[k % 2]
                for c in range(2):
                    mm(fw, pb[:, :], wkn[:, c, h, :], ckvn[:, c, ts], c == 0, c == 1, [bwkn, bckvn[tt]], bpb)
                o, bo = ost[k % 2], bost[k % 2]
                if k % 2 == 0:
                    fw.op("dve", lambda e, o=o, pb=pb: e.tensor_copy(o[:], pb[:, :]), reads=[bpb], writes=[bo])
                else:
                    fw.op("act", lambda e, o=o, pb=pb: e.copy(o[:], pb[:, :]), reads=[bpb], writes=[bo])
                fw.op("sp", lambda e, o=o, h=h, ts=ts: e.dma_start(out=knT_d[h * 128:(h + 1) * 128, ts], in_=o[:]),
                      reads=[bo], dma=bo)
                k += 1
        vst = [cx.sb("kv_vo%d" % i, [128, H2, 128], BF16) for i in range(2)]
        bvst = fw.bufs("kv_vo", 2)
        for l in range(NL):
            o, bo = vst[l % 2], bvst[l % 2]
            for half in range(2):
                pb, bpb = cm.bank[2 + half], cm.bbank[2 + half]
                for c in range(2):
                    mm(fw, pb[:, :], ckvn[:, c, l * 128:(l + 1) * 128],
                       wvv[:, c, half * 4:half * 4 + 4, :].rearrange("p h d -> p (h d)"),
                       c == 0, c == 1, [bwvv, bckvn[l // 4]], bpb)
                if half == 0:
                    fw.op("dve", lambda e, o=o, pb=pb: e.tensor_copy(
                        o[:, 0:4, :], pb[:, :].rearrange("p (h d) -> p h d", h=4)),
                        reads=[bpb], writes=[bo])
                else:
                    fw.op("act", lambda e, o=o, pb=pb: e.copy(
                        o[:, 4:8, :], pb[:, :].rearrange("p (h d) -> p h d", h=4)),
                        reads=[bpb], pwrites=[bo])
            fw.op("sp", lambda e, o=o, l=l: e.dma_start(
                out=v2_d[:, :, l, :].rearrange("h p d -> p h d"), in_=o[:]), reads=[bo], dma=bo)


def build_p2(cfg):
    cx = Ctx()
    fw = cx.fw
    cm = Common(cx, cfg)
    cm_eps(cx, cm)
    T, NL, S = cfg.T, cfg.NL, cfg.S
    NSEG = NL // 2
    x_d = cx.din("x", [T, D], F32)
    masks_d = cx.din("masks", [128, 2, 4, 128], BF16)
    tri_d = cx.din("tri", [16 * NSEG, 16 * NSEG], F32)
    seln_d = cx.din("selneg", [16 * NSEG, 4], F32)
    qT_d = cx.din("qT", [D, T], BF16)
    kT_all_d = cx.din("kT_all", [R, D, T], BF16)
    v_all_d = cx.din("v_all", [R, H1, 128, NL, 65], BF16)
    lf_all_d = cx.din("lf_all", [R, 16 * NSEG, 2, 128], F32)
    wout_d = cx.din("w_out", [D, D], F32)
    gffn_d = cx.din("g_ffn", [128, NCH], F32)
    wup_d = cx.din("w_up", [D, DFF], F32)
    wdn_d = cx.din("w_down", [DFF, D], F32)
    gkv_d = cx.din("g_kv", [128, NCH], F32)
    wkva_d = cx.din("w_kv_a", [D, 320], F32)
    gkva_d = cx.din("g_kva", [128, 2], F32)
    wkvb_d = cx.din("w_kv_b", [256, 2048], F32)
    cs1_d = cx.din("cs1", [128, T], F32)
    cs2_d = cx.din("cs2", [128, T], F32)
    x1T_d = cx.dout("x1T", [128, NCH, T], F32)
    knT_d = cx.dout("knT", [H2 * 128, T], BF16)
    krT_d = cx.dout("krT", [64, T], BF16)
    v2_d = cx.dout("v2", [H2, 128, NL, 128], BF16)
    caug_d = cx.dint("caug", [3, R, 16 * NSEG, 2, 128], BF16)
    cq_d = cx.dint("cq", [3, 16 * NSEG, 2, 128], BF16)

    xT = cx.sb("xT_sb", [128, NCH, T], F32)
    bxT = fw.bufs("xT", cfg.NTT)
    load_xT_from_tokens(cx, cm, x_d, xT, bxT)
    load_masks(cx, cm, masks_d)
    bcaug, bcq = fox_cumgate(cx, cm, lf_all_d, tri_d, seln_d, caug_d, cq_d)
    with cx.scope():
        CTX = cx.sb("CTX", [64, H1, T], BF16)
        bCTX = fw.buf("CTX")
        with cx.scope():
            spec = FoxSpec(cx, cm, qT_d, kT_all_d, v_all_d, caug_d, cq_d, bcaug, bcq, CTX, bCTX)
            run_attention(cx, cm, spec)
        out_proj(cx, cm, CTX, bCTX, 64, H1, wout_d, xT, bxT, "fo")
    ffn(cx, cm, xT, bxT, gffn_d, wup_d, wdn_d, "f0")
    mla_kv(cx, cm, xT, bxT, gkv_d, wkva_d, gkva_d, wkvb_d, cs1_d, cs2_d, knT_d, krT_d, v2_d)
    for c in range(NCH):
        fw.op("sp", lambda e, c=c: e.dma_start(out=x1T_d[:, c, :], in_=xT[:, c, :]), reads=bxT, dma=bxT[0])
    return cx.finish()


def rope_tables_np(pos):
    inv = 1.0 / (10000.0 ** (np.arange(0, 64, 2, dtype=np.float32) / 64.0))
    ang = pos.astype(np.float32)[None, :] * inv[:, None].astype(np.float32)
    c, s = np.cos(ang).astype(np.float32), np.sin(ang).astype(np.float32)
    cs1 = np.concatenate([c, c], 0)
    cs2 = np.concatenate([-s, s], 0)
    return (np.ascontiguousarray(np.concatenate([cs1, cs1], 0)),
            np.ascontiguousarray(np.concatenate([cs2, cs2], 0)))


def core_masks(j):
    m = np.zeros((128, 2, 4, 128), np.float32)
    s = np.arange(128)[:, None]
    t = np.arange(128)[None, :]
    tri = np.where(s <= t, 0.0, NEG).astype(np.float32)
    for r in range(R):
        m[:, 0, r, :] = 0.0 if r < j else (tri if r == j else NEG)
        m[:, 1, r, :] = 0.0 if r > j else (tri if r == j else NEG)
    return m.astype(ml_dtypes.bfloat16)


def tri_np(nseg):
    P = 16 * nseg
    p = np.arange(P)
    return ((p[:, None] // nseg == p[None, :] // nseg) & (p[:, None] % nseg < p[None, :] % nseg)).astype(np.float32)


def p2_inputs(cfg, inp, c, p1_outs):
    b, j = c // R, c % R
    tok = core_tokens(cfg, j)
    NSEG = cfg.NL // 2
    grp = [p1_outs[b * R + r] for r in range(R)]
    seln = np.zeros((16 * NSEG, 4), np.float32)
    seln[:, j] = -1.0
    cs1, cs2 = rope_tables_np(tok)
    return {
        "ident": _IDENT,
        "x": np.ascontiguousarray(inp["x"][b][tok]),
        "masks": core_masks(j),
        "tri": tri_np(NSEG),
        "selneg": seln,
        "qT": p1_outs[c]["qT"],
        "kT_all": np.stack([g["kT"] for g in grp]),
        "v_all": np.stack([g["v"] for g in grp]),
        "lf_all": np.stack([g["logf"].reshape(16 * NSEG, 2, 128) for g in grp]),
        "w_out": np.ascontiguousarray(inp["fox_w_out"][0]),
        "g_ffn": gcols(inp["norm_ffn_g"][0]),
        "w_up": np.ascontiguousarray(inp["ffn_w_up"][0]),
        "w_down": np.ascontiguousarray(inp["ffn_w_down"][0]),
        "g_kv": gcols(inp["kv_norm_g"]),
        "w_kv_a": np.ascontiguousarray(inp["mla_w_kv_a"]),
        "g_kva": gcols(inp["mla_kv_a_norm_g"]),
        "w_kv_b": np.ascontiguousarray(inp["mla_w_kv_b"]),
        "cs1": cs1,
        "cs2": cs2,
    }


def mla_q(cx, cm, x1T_d, g_d, wqa_d, gqa_d, wqb_d, cs1_d, cs2_d, qn_d, qr_d):
    cfg, fw = cm.cfg, cx.fw
    T = cfg.T
    bqn, bqr = fw.buf("qn_d"), fw.buf("qr_d")
    with cx.scope():
        xT = cx.sb("xTq", [128, NCH, T], F32)
        bxT = fw.bufs("xTq", cfg.NTT)
        for c in range(NCH):
            fw.op("sp", lambda e, c=c: e.dma_start(out=xT[:, c, :], in_=x1T_d[:, c, :]), pwrites=bxT, dma=bxT[0])
        gcol, bg = load_gcol(cx, "q_g", g_d)
        xn = cx.sb("q_xn", [128, NCH, T], BF16)
        bxn = fw.bufs("q_xn", cfg.NTT)
        rmsnorm_T(cx, cm, xT, bxT, NCH, D, gcol, bg, xn, bxn, "q_n")
        wa, bwa = wload(cx, "q_wa", [128, NCH, 384], wqa_d.rearrange("(c p) n -> p c n", p=128), npieces=2)
        qa = cx.sb("q_qa", [128, 3, T], F32)
        bqa = fw.bufs("q_qa", cfg.NTT)
        k = 0
        for tt in range(cfg.NTT):
            ts = slice(tt * 512, (tt + 1) * 512)
            for m in range(3):
                pb, bpb = cm.bank[k % 2], cm.bbank[k % 2]
                k += 1
                for c in range(NCH):
                    mm(fw, pb[:, :], wa[:, c, m * 128:(m + 1) * 128], xn[:, c, ts], c == 0, c == NCH - 1,
                       [bwa, bxn[tt]], bpb)
                fw.op("act", lambda e, pb=pb, m=m, ts=ts: e.copy(qa[:, m, ts], pb[:, :]),
                      reads=[bpb], pwrites=[bqa[tt]])
        gq, bgq = load_gcol(cx, "qa_g", gqa_d)
        cqn = cx.sb("q_cqn", [128, 3, T], BF16)
        bcqn = fw.bufs("q_cqn", cfg.NTT)
        rmsnorm_T(cx, cm, qa, bqa, 3, 384, gq, bgq, cqn, bcqn, "qa_n")
        wb4 = wqb_d.rearrange("(c p) (h x) -> p c h x", p=128, x=192)
        wqn = cx.sb("q_wqn", [128, 3, H2, 128], BF16)
        wqr = cx.sb("q_wqr", [128, 3, H2, 64], BF16)
        wqs = cx.sb("q_wqs", [128, 3, H2, 64], BF16)
        bwqn, bwqr, bwqs = fw.buf("q_wqn"), fw.buf("q_wqr"), fw.buf("q_wqs")
        for c in range(3):
            fw.op("pool", lambda e, c=c: e.dma_start(out=wqn[:, c, :, :], in_=wb4[:, c, :, 0:128]), pwrites=[bwqn], dma=bwqn)
            fw.op("pool", lambda e, c=c: e.dma_start(out=wqr[:, c, :, :], in_=wb4[:, c, :, 128:192]), pwrites=[bwqr], dma=bwqr)
            fw.op("pool", lambda e, c=c: e.dma_start(out=wqs[:, c, :, 0:32], in_=wb4[:, c, :, 160:192]), pwrites=[bwqs], dma=bwqs)
            fw.op("pool", lambda e, c=c: e.dma_start(out=wqs[:, c, :, 32:64], in_=wb4[:, c, :, 128:160]), pwrites=[bwqs], dma=bwqs)
        cs1 = cx.sb("q_cs1", [128, T], F32)
        cs2 = cx.sb("q_cs2", [128, T], F32)
        bcs = fw.buf("q_cs")
        fw.op("sp", lambda e: e.dma_start(out=cs1[:], in_=cs1_d[:, :]), pwrites=[bcs], dma=bcs)
        fw.op("sp", lambda e: e.dma_start(out=cs2[:], in_=cs2_d[:, :]), pwrites=[bcs], dma=bcs)
        ost = [cx.sb("q_o%d" % i, [128, 512], BF16) for i in range(2)]
        bost = fw.bufs("q_o", 2)
        k = 0
        for h in range(H2):
            for tt in range(cfg.NTT):
                ts = slice(tt * 512, (tt + 1) * 512)
                pb, bpb = cm.bank[k % 2], cm.bbank[k % 2]
                for c in range(3):
                    mm(fw, pb[:, :], wqn[:, c, h, :], cqn[:, c, ts], c == 0, c == 2, [bwqn, bcqn[tt]], bpb)
                o, bo = ost[k % 2], bost[k % 2]
                if k % 2 == 0:
                    fw.op("dve", lambda e, o=o, pb=pb: e.tensor_copy(o[:], pb[:, :]), reads=[bpb], writes=[bo])
                else:
                    fw.op("act", lambda e, o=o, pb=pb: e.copy(o[:], pb[:, :]), reads=[bpb], writes=[bo])
                fw.op("sp", lambda e, o=o, h=h, ts=ts: e.dma_start(out=qn_d[h * 128:(h + 1) * 128, ts], in_=o[:]),
                      reads=[bo], pwrites=[bqn], dma=bo)
                k += 1
        t1 = cx.sb("q_t1", [128, 512], F32)
        t2 = cx.sb("q_t2", [128, 512], F32)
        bt1, bt2 = fw.buf("q_t1"), fw.buf("q_t2")
        rst = [cx.sb("q_ro%d" % i, [128, 512], BF16) for i in range(2)]
        brst = fw.bufs("q_ro", 2)
        k = 0
        for hp in range(H2 // 2):
            for tt in range(cfg.NTT):
                ts = slice(tt * 512, (tt + 1) * 512)
                pa, bpa = cm.bank[2], cm.bbank[2]
                pc, bpc = cm.bank[3], cm.bbank[3]
                for c in range(3):
                    mm(fw, pa[:, :], wqr[:, c, 2 * hp:2 * hp + 2, :].rearrange("p h d -> p (h d)"), cqn[:, c, ts],
                       c == 0, c == 2, [bwqr, bcqn[tt]], bpa)
                for c in range(3):
                    mm(fw, pc[:, :], wqs[:, c, 2 * hp:2 * hp + 2, :].rearrange("p h d -> p (h d)"), cqn[:, c, ts],
                       c == 0, c == 2, [bwqs, bcqn[tt]], bpc)
                o, bo = rst[k % 2], brst[k % 2]
                k += 1
                fw.op("dve", lambda e, ts=ts: e.tensor_tensor(out=t1[:], in0=pa[:, :], in1=cs1[:, ts], op=ALU.mult),
                      reads=[bpa, bcs], writes=[bt1])
                fw.op("dve", lambda e, ts=ts: e.tensor_tensor(out=t2[:], in0=pc[:, :], in1=cs2[:, ts], op=ALU.mult),
                      reads=[bpc, bcs], writes=[bt2])
                fw.op("dve", lambda e, o=o: e.tensor_tensor(out=o[:], in0=t1[:], in1=t2[:], op=ALU.add),
                      reads=[bt1, bt2], writes=[bo])
                fw.op("sp", lambda e, o=o, hp=hp, ts=ts: e.dma_start(out=qr_d[hp * 128:(hp + 1) * 128, ts], in_=o[:]),
                      reads=[bo], pwrites=[bqr], dma=bo)
    return bqn, bqr


class MlaSpec:
    nheads = H2

    def __init__(self, cx, cm, qn_d, qr_d, bqn, bqr, knT_all_d, krT_all_d, v2_all_d, CTX, bCTX):
        cfg, fw = cm.cfg, cx.fw
        self.cx, self.cm, self.cfg, self.fw = cx, cm, cfg, fw
        T, S, NL = cfg.T, cfg.S, cfg.NL
        self.qn_d, self.qr_d, self.bqn, self.bqr = qn_d, qr_d, bqn, bqr
        self.knT_all_d, self.v2_all_d = knT_all_d, v2_all_d
        self.CTX, self.bCTX = CTX, bCTX
        self.KN = [cx.sb("KN%d" % i, [128, S], BF16) for i in range(2)]
        self.V2 = [cx.sb("V2_%d" % i, [128, R * NL, 128], BF16) for i in range(2)]
        self.QN = [cx.sb("QN%d" % i, [128, T], BF16) for i in range(2)]
        self.QR = [cx.sb("QR%d" % i, [64, T], BF16) for i in range(2)]
        self.bKN, self.bV2 = fw.bufs("KN", 2), fw.bufs("V2_", 2)
        self.bQN, self.bQR = fw.bufs("QN", 2), fw.bufs("QR", 2)
        self.KR = cx.sb("KR", [64, S], BF16)
        self.bKR = fw.buf("KR")
        KR = self.KR
        fw.op("sp", lambda e: e.dma_start(out=KR[:, :].rearrange("p (r t) -> p r t", r=R),
                                          in_=krT_all_d.rearrange("r p t -> p r t")),
              writes=[self.bKR], dma=self.bKR)
        self.OUT = [cm.bank[4], cm.bank[5]]
        self.bOUT = [cm.bbank[4], cm.bbank[5]]
        self.DEN = [cm.bank[6], cm.bank[7]]
        self.bDEN = [cm.bbank[6], cm.bbank[7]]
        self.rden = [cx.sb("rden2_%d" % i, [128, 512], F32) for i in range(2)]
        self.brden = fw.bufs("rden2_", 2)
        self.scale = float(192 ** -0.5)

    def load_head(self, h):
        fw = self.fw
        s = h % 2
        KN, V2, QN, QR = self.KN[s], self.V2[s], self.QN[s], self.QR[s]
        fw.op("sp", lambda e: e.dma_start(
            out=KN[:, :].rearrange("p (r t) -> p r t", r=R),
            in_=self.knT_all_d[:, h * 128:(h + 1) * 128, :].rearrange("r p t -> p r t")),
            writes=[self.bKN[s]], dma=self.bKN[s])
        fw.op("sp", lambda e: e.dma_start(
            out=V2[:, :, :].rearrange("p (r l) d -> p r (l d)", r=R),
            in_=self.v2_all_d[:, h, :, :, :].rearrange("r p l d -> p r (l d)")),
            writes=[self.bV2[s]], dma=self.bV2[s])
        fw.op("sp", lambda e: e.dma_start(out=QN[:, :], in_=self.qn_d[h * 128:(h + 1) * 128, :]),
              reads=[self.bqn], writes=[self.bQN[s]], dma=self.bQN[s])
        fw.op("sp", lambda e: e.dma_start(out=QR[:, :], in_=self.qr_d[h * 64:(h + 1) * 64, :]),
              reads=[self.bqr], writes=[self.bQR[s]], dma=self.bQR[s])

    def qk(self, h, u, unit, ss_ap, bss):
        fw, cm = self.fw, self.cm
        s = h % 2
        kb, c0, msk = unit
        q0 = u * 512 + c0
        q1 = (u + 1) * 512
        mm(fw, ss_ap[:, c0:512], self.KN[s][:, kb * 128:(kb + 1) * 128], self.QN[s][:, q0:q1], True, False,
           [self.bKN[s], self.bQN[s]], bss)
        mm(fw, ss_ap[:, c0:512], self.KR[:, kb * 128:(kb + 1) * 128], self.QR[s][:, q0:q1], False, msk is None,
           [self.bKR, self.bQR[s]], bss)
        if msk is not None:
            mm(fw, ss_ap[:, c0:c0 + 128], cm.identb[:, :], cm.masks[:, msk[0], msk[1], :], False, True,
               [cm.b_identb, cm.b_masks], bss)

    def expf(self, out_ap, in_ap, bins, bpt, first):
        sc = self.scale
        self.fw.op("act", lambda e: e.activation(out=out_ap, in_=in_ap, func=AF.Exp, scale=sc),
                   reads=bins, writes=[bpt] if first else [], pwrites=[] if first else [bpt])

    def pv(self, h, u, unit, pt_ap, bpt, first, last):
        s = h % 2
        ob = (h * self.cfg.NQ + u) % 2
        kb, c0, msk = unit
        mm(self.fw, self.OUT[ob][:, c0:512], self.V2[s][:, kb, :], pt_ap, first, last,
           [self.bV2[s], bpt], self.bOUT[ob])
        mm(self.fw, self.DEN[ob][:, c0:512], self.cm.onesb[:, :], pt_ap, first, last,
           [self.cm.b_onesb, bpt], self.bDEN[ob])

    def epi1(self, h, u):
        ob = (h * self.cfg.NQ + u) % 2
        rd, DEN = self.rden[ob], self.DEN[ob]
        self.fw.op("dve", lambda e: e.reciprocal(rd[:, :], DEN[:, :]), reads=[self.bDEN[ob]], writes=[self.brden[ob]])

    def epi2(self, h, u):
        ob = (h * self.cfg.NQ + u) % 2
        rd, OUT, CTX = self.rden[ob], self.OUT[ob], self.CTX
        self.fw.op("dve", lambda e: e.tensor_tensor(out=CTX[:, h, u * 512:(u + 1) * 512], in0=OUT[:, :],
                                                    in1=rd[:, :], op=ALU.mult),
                   reads=[self.bOUT[ob], self.brden[ob]], pwrites=[self.bCTX])


def final_norm_store(cx, cm, xT, bxT, g_d, out_d):
    cfg, fw = cm.cfg, cx.fw
    T = cfg.T
    with cx.scope():
        gcol, bg = load_gcol(cx, "fin_g", g_d)
        yT = cx.sb("yT", [128, NCH, T], F32)
        byT = fw.bufs("yT", cfg.NTT)
        rmsnorm_T(cx, cm, xT, bxT, NCH, D, gcol, bg, yT, byT, "fin_n")
        stg = [cx.sb("ostg%d" % i, [128, D], F32) for i in range(2)]
        bstg = fw.bufs("ostg", 2)
        k = 0
        for l in range(cfg.NL):
            s, bs = stg[l % 2], bstg[l % 2]
            for half in range(2):
                pb, bpb = cm.bank[k % 2], cm.bbank[k % 2]
                k += 1
                for q in range(4):
                    c = half * 4 + q
                    fw.op("pe", lambda e, pb=pb, c=c, q=q, l=l: e.transpose(
                        pb[:, q * 128:(q + 1) * 128], yT[:, c, l * 128:(l + 1) * 128], cm.ident[:]),
                        reads=[byT[l // 4], cm.b_ident], writes=[bpb] if q == 0 else [], pwrites=[] if q == 0 else [bpb])
                if half == 0:
                    fw.op("dve", lambda e, s=s, pb=pb: e.tensor_copy(s[:, 0:512], pb[:, :]),
                          reads=[bpb], writes=[bs])
                else:
                    fw.op("act", lambda e, s=s, pb=pb: e.copy(s[:, 512:1024], pb[:, :]),
                          reads=[bpb], pwrites=[bs])
            fw.op("sp", lambda e, s=s, l=l: e.dma_start(out=out_d[l * 128:(l + 1) * 128, :], in_=s[:]),
                  reads=[bs], dma=bs)


def build_p3(cfg):
    cx = Ctx()
    fw = cx.fw
    cm = Common(cx, cfg)
    cm_eps(cx, cm)
    T, NL = cfg.T, cfg.NL
    masks_d = cx.din("masks", [128, 2, 4, 128], BF16)
    x1T_d = cx.din("x1T", [128, NCH, T], F32)
    gmix_d = cx.din("g_mix", [128, NCH], F32)
    wqa_d = cx.din("w_q_a", [D, 384], F32)
    gqa_d = cx.din("g_qa", [128, 3], F32)
    wqb_d = cx.din("w_q_b", [384, 1536], F32)
    knT_all_d = cx.din("knT_all", [R, H2 * 128, T], BF16)
    krT_all_d = cx.din("krT_all", [R, 64, T], BF16)
    v2_all_d = cx.din("v2_all", [R, H2, 128, NL, 128], BF16)
    cs1_d = cx.din("cs1", [128, T], F32)
    cs2_d = cx.din("cs2", [128, T], F32)
    wout_d = cx.din("w_out", [D, D], F32)
    gffn_d = cx.din("g_ffn", [128, NCH], F32)
    wup_d = cx.din("w_up", [D, DFF], F32)
    wdn_d = cx.din("w_down", [DFF, D], F32)
    gfin_d = cx.din("g_fin", [128, NCH], F32)
    out_d = cx.dout("out", [T, D], F32)
    qn_d = cx.dint("qn_s", [H2 * 128, T], BF16)
    qr_d = cx.dint("qr_s", [H2 * 64, T], BF16)

    load_masks(cx, cm, masks_d)
    bqn, bqr = mla_q(cx, cm, x1T_d, gmix_d, wqa_d, gqa_d, wqb_d, cs1_d, cs2_d, qn_d, qr_d)
    with cx.scope():
        CTX = cx.sb("CTX2", [128, H2, T], BF16)
        bCTX = fw.buf("CTX2")
        with cx.scope():
            spec = MlaSpec(cx, cm, qn_d, qr_d, bqn, bqr, knT_all_d, krT_all_d, v2_all_d, CTX, bCTX)
            run_attention(cx, cm, spec)
        xT = cx.sb("xT_sb", [128, NCH, T], F32)
        bxT = fw.bufs("xT", cfg.NTT)
        for c in range(NCH):
            fw.op("sp", lambda e, c=c: e.dma_start(out=xT[:, c, :], in_=x1T_d[:, c, :]), pwrites=bxT, dma=bxT[0])
        out_proj(cx, cm, CTX, bCTX, 128, H2, wout_d, xT, bxT, "mo")
        ffn(cx, cm, xT, bxT, gffn_d, wup_d, wdn_d, "f1")
        final_norm_store(cx, cm, xT, bxT, gfin_d, out_d)
    return cx.finish()


def p3_inputs(cfg, inp, c, p2_outs):
    b, j = c // R, c % R
    tok = core_tokens(cfg, j)
    grp = [p2_outs[b * R + r] for r in range(R)]
    cs1, cs2 = rope_tables_np(tok)
    return {
        "ident": _IDENT,
        "masks": core_masks(j),
        "x1T": p2_outs[c]["x1T"],
        "g_mix": gcols(inp["norm_mix_g"][1]),
        "w_q_a": np.ascontiguousarray(inp["mla_w_q_a"][0]),
        "g_qa": gcols(inp["mla_q_a_norm_g"][0]),
        "w_q_b": np.ascontiguousarray(inp["mla_w_q_b"][0]),
        "knT_all": np.stack([g["knT"] for g in grp]),
        "krT_all": np.stack([g["krT"] for g in grp]),
        "v2_all": np.stack([g["v2"] for g in grp]),
        "cs1": cs1,
        "cs2": cs2,
        "w_out": np.ascontiguousarray(inp["mla_w_out"][0]),
        "g_ffn": gcols(inp["norm_ffn_g"][1]),
        "w_up": np.ascontiguousarray(inp["ffn_w_up"][1]),
        "w_down": np.ascontiguousarray(inp["ffn_w_down"][1]),
        "g_fin": gcols(inp["final_norm_g"]),
    }


_NC_CACHE = {}


def _get_nc(name, cfg):
    key = (name, cfg.S)
    if key not in _NC_CACHE:
        _NC_CACHE[key] = {"p1": build_p1, "p2": build_p2, "p3": build_p3}[name](cfg)
    return _NC_CACHE[key]


def kernel(**inputs):
    inp = {k: np.asarray(v) for k, v in inputs.items()}
    B, S, _ = inp["x"].shape
    cfg = Cfg(S)
    ncores = B * R
    cores = list(range(ncores))
    r1 = run_bass_kernel_spmd(build_p1(cfg), [p1_inputs(cfg, inp, c) for c in cores], core_ids=cores).results
    r2 = run_bass_kernel_spmd(build_p2(cfg), [p2_inputs(cfg, inp, c, r1) for c in cores], core_ids=cores).results
    r3 = run_bass_kernel_spmd(build_p3(cfg), [p3_inputs(cfg, inp, c, r2) for c in cores], core_ids=cores).results
    out = np.zeros((B, S, D), np.float32)
    for c in cores:
        b, j = c // R, c % R
        out[b, core_tokens(cfg, j)] = np.asarray(r3[c]["out"], np.float32)
    return out
```
